# Optimizing a Trainium2 kernel written in Bass

```python
import jax, jax.numpy as jnp
from jax import lax
import numpy as np

D_MODEL = 2048
BATCH = 1
SEQ = 8192
DEPTH = 4

HEAD_DIM = 64
RWKV_HEADS = 12
RWKV_WIDTH = RWKV_HEADS * HEAD_DIM
DECAY_LORA = 64
ICLR_LORA = 64
GATE_LORA = 128
ATTN_Q_HEADS = 12
ATTN_KV_HEADS = 4
ATTN_GROUP = ATTN_Q_HEADS // ATTN_KV_HEADS
ATTN_WIDTH = ATTN_Q_HEADS * HEAD_DIM
WINDOW = 128
GM_HEADS = 4
GM_HEAD_DIM = 128
GM_WIDTH = GM_HEADS * GM_HEAD_DIM
GM_CHUNK = 128
MIX_WIDTH = RWKV_WIDTH + ATTN_WIDTH + GM_WIDTH
D_FF = 4 * D_MODEL
RWKV_COLS = 3 * RWKV_WIDTH + DECAY_LORA + ICLR_LORA + GATE_LORA
ATTN_COLS = ATTN_WIDTH + 2 * ATTN_KV_HEADS * HEAD_DIM
GM_COLS = 2 * GM_WIDTH
IN_COLS = RWKV_COLS + ATTN_COLS + GM_COLS
NORM_EPS = 1e-6
GN_EPS = 64e-5

kernel_name = "hymba_style_rwkv7_swa_gmlp_trunk"


def _rmsnorm(x, g):
    xf = x.astype(jnp.float32)
    y = xf * lax.rsqrt(jnp.mean(xf * xf, axis=-1, keepdims=True) + NORM_EPS)
    return (y * g.astype(jnp.float32)).astype(x.dtype)


def _layernorm(x, g, b):
    xf = x.astype(jnp.float32)
    mu = jnp.mean(xf, axis=-1, keepdims=True)
    var = jnp.mean(jnp.square(xf - mu), axis=-1, keepdims=True)
    y = (xf - mu) * lax.rsqrt(var + NORM_EPS)
    return (y * g.astype(jnp.float32) + b.astype(jnp.float32)).astype(x.dtype)


def _rwkv7_scan(r, decay, k, v, kk, a):
    B, S, H, N = r.shape
    xs = tuple(jnp.moveaxis(t, 1, 0) for t in (r, decay, k, v, -kk, kk * a))

    def step(state, inp):
        r_t, w_t, k_t, v_t, a_t, b_t = inp
        sa = jnp.einsum('bhij,bhj->bhi', state, a_t)
        state = (state * w_t[:, :, None, :] + sa[..., None] * b_t[:, :, None, :]
                 + v_t[..., None] * k_t[:, :, None, :])
        y_t = jnp.einsum('bhij,bhj->bhi', state, r_t)
        return state, y_t

    state0 = jnp.zeros((B, H, N, N), jnp.float32)
    _, y = lax.scan(step, state0, xs)
    return jnp.moveaxis(y, 0, 1)


def _rwkv7_mix(p, mu, w0, decay_up, a0, a_up, g_up, k_k, k_a, r_k, lnx_g, lnx_b):
    B, S, _ = p.shape
    f32 = jnp.float32
    prev = jnp.pad(p, ((0, 0), (1, 0), (0, 0)))[:, :-1]
    p = p + (prev - p) * mu
    cuts = list(np.cumsum([RWKV_WIDTH, RWKV_WIDTH, RWKV_WIDTH, DECAY_LORA, ICLR_LORA]))
    r, k, v, xw, xa, xg = jnp.split(p, cuts, axis=-1)
    w = -jax.nn.softplus(-(w0 + jnp.tanh(xw) @ decay_up)) - 0.5
    decay = jnp.exp(-jnp.exp(w.astype(f32)))
    a = jax.nn.sigmoid(a0 + xa @ a_up)
    g = jax.nn.sigmoid(xg) @ g_up
    heads = lambda t: t.astype(f32).reshape(B, S, RWKV_HEADS, HEAD_DIM)
    kk = heads(k * k_k)
    kk = kk / jnp.maximum(jnp.sqrt(jnp.sum(kk * kk, axis=-1, keepdims=True)), 1e-12)
    k = k * (1.0 + (a - 1.0) * k_a)
    rh, kh, vh = heads(r), heads(k), heads(v)
    y = _rwkv7_scan(rh, heads(decay), kh, vh, kk, heads(a))
    m = jnp.mean(y, axis=-1, keepdims=True)
    var = jnp.mean(jnp.square(y - m), axis=-1, keepdims=True)
    y = ((y - m) * lax.rsqrt(var + GN_EPS)).reshape(B, S, RWKV_WIDTH)
    y = y * lnx_g.astype(f32) + lnx_b.astype(f32)
    bonus = jnp.sum(rh * kh * r_k.astype(f32), axis=-1, keepdims=True) * vh
    y = y + bonus.reshape(B, S, RWKV_WIDTH)
    return (y * g.astype(f32)).astype(p.dtype)


def _swa_sinks(p, sinks):
    B, S, _ = p.shape
    nb = S // WINDOW
    kvw = ATTN_KV_HEADS * HEAD_DIM
    q, k, v = jnp.split(p, [ATTN_WIDTH, ATTN_WIDTH + kvw], axis=-1)
    q = q.reshape(B, nb, WINDOW, ATTN_KV_HEADS, ATTN_GROUP, HEAD_DIM)
    k = k.reshape(B, nb, WINDOW, ATTN_KV_HEADS, HEAD_DIM)
    v = v.reshape(B, nb, WINDOW, ATTN_KV_HEADS, HEAD_DIM)

    def with_prev(t):
        prev = jnp.concatenate([jnp.zeros_like(t[:, :1]), t[:, :-1]], axis=1)
        return jnp.concatenate([prev, t], axis=2)

    kb, vb = with_prev(k), with_prev(v)
    s = jnp.einsum('bnqhgd,bnkhd->bnhgqk', q, kb).astype(jnp.float32) * (HEAD_DIM ** -0.5)
    i = jnp.arange(WINDOW)[:, None]
    j = jnp.arange(2 * WINDOW)[None, :]
    band = (j > i) & (j <= i + WINDOW)
    valid = (jnp.arange(nb)[:, None, None] > 0) | (j >= WINDOW)[None]
    mask = band[None] & valid
    s = jnp.where(mask[None, :, None, None], s, -jnp.inf)
    sink = jnp.broadcast_to(
        sinks.astype(jnp.float32).reshape(1, 1, ATTN_KV_HEADS, ATTN_GROUP, 1, 1),
        s.shape[:-1] + (1,))
    prob = jax.nn.softmax(jnp.concatenate([s, sink], axis=-1), axis=-1)[..., :-1]
    o = jnp.einsum('bnhgqk,bnkhd->bnqhgd', prob.astype(vb.dtype), vb)
    return o.reshape(B, S, ATTN_WIDTH)


def _chunk_gmlp(p, ln_g, ln_b, ws, bs):
    B, S, _ = p.shape
    nc = S // GM_CHUNK
    z = jax.nn.gelu(p, approximate=False)
    u, z2 = jnp.split(z, 2, axis=-1)
    z2 = _layernorm(z2, ln_g, ln_b).reshape(B, nc, GM_CHUNK, GM_HEADS, GM_HEAD_DIM)
    causal = jnp.tril(jnp.ones((GM_CHUNK, GM_CHUNK), dtype=bool))
    w = jnp.where(causal[None], ws, jnp.zeros_like(ws))
    mixed = jnp.einsum('hts,bcshe->bcthe', w, z2) + bs.T[:, :, None]
    u = u.reshape(B, nc, GM_CHUNK, GM_HEADS, GM_HEAD_DIM)
    return (u * mixed).reshape(B, S, GM_WIDTH)


def setup_inputs(seed: int = 0) -> dict:
    key = jax.random.key(seed)
    ks = jax.random.split(key, 32)
    L, D = DEPTH, D_MODEL
    nrm = lambda k, shape, scale: jax.random.normal(k, shape, jnp.float32) * scale
    return {
        "x": nrm(ks[0], (BATCH, SEQ, D), 1.0),
        "ln1_g": 1.0 + nrm(ks[1], (L, D), 0.02),
        "w_in": nrm(ks[2], (L, D, IN_COLS), D ** -0.5),
        "rwkv_mu": jax.random.uniform(ks[3], (L, RWKV_COLS), jnp.float32),
        "rwkv_w0": jax.random.uniform(ks[4], (L, RWKV_WIDTH), jnp.float32, -5.0, 0.5),
        "rwkv_decay_up": nrm(ks[5], (L, DECAY_LORA, RWKV_WIDTH), 0.1 * DECAY_LORA ** -0.5),
        "rwkv_a0": nrm(ks[6], (L, RWKV_WIDTH), 0.1),
        "rwkv_a_up": nrm(ks[7], (L, ICLR_LORA, RWKV_WIDTH), 0.1 * ICLR_LORA ** -0.5),
        "rwkv_g_up": nrm(ks[8], (L, GATE_LORA, RWKV_WIDTH), GATE_LORA ** -0.5),
        "rwkv_k_k": 0.85 + nrm(ks[9], (L, RWKV_WIDTH), 0.05),
        "rwkv_k_a": 1.0 + nrm(ks[10], (L, RWKV_WIDTH), 0.05),
        "rwkv_r_k": nrm(ks[11], (L, RWKV_HEADS, HEAD_DIM), 0.1),
        "rwkv_lnx_g": 1.0 + nrm(ks[12], (L, RWKV_WIDTH), 0.02),
        "rwkv_lnx_b": nrm(ks[13], (L, RWKV_WIDTH), 0.02),
        "attn_sinks": nrm(ks[14], (L, ATTN_Q_HEADS), 1.0),
        "attn_norm_g": 1.0 + nrm(ks[15], (L, ATTN_WIDTH), 0.02),
        "gm_ln_g": 1.0 + nrm(ks[16], (L, GM_WIDTH), 0.02),
        "gm_ln_b": nrm(ks[17], (L, GM_WIDTH), 0.02),
        "gm_ws": nrm(ks[18], (L, GM_HEADS, GM_CHUNK, GM_CHUNK), GM_CHUNK ** -0.5),
        "gm_bs": 1.0 + nrm(ks[19], (L, GM_HEADS, GM_CHUNK), 0.1),
        "gm_norm_g": 1.0 + nrm(ks[20], (L, GM_WIDTH), 0.02),
        "w_out": nrm(ks[21], (L, MIX_WIDTH, D), MIX_WIDTH ** -0.5),
        "ln2_g": 1.0 + nrm(ks[22], (L, D), 0.02),
        "w_ffn_up": nrm(ks[23], (L, D, D_FF), D ** -0.5),
        "w_ffn_down": nrm(ks[24], (L, D_FF, D), D_FF ** -0.5),
        "lnf_g": 1.0 + nrm(ks[25], (D,), 0.02),
    }


def reference(x, ln1_g, w_in, rwkv_mu, rwkv_w0, rwkv_decay_up, rwkv_a0, rwkv_a_up,
              rwkv_g_up, rwkv_k_k, rwkv_k_a, rwkv_r_k, rwkv_lnx_g, rwkv_lnx_b,
              attn_sinks, attn_norm_g, gm_ln_g, gm_ln_b, gm_ws, gm_bs, gm_norm_g,
              w_out, ln2_g, w_ffn_up, w_ffn_down, lnf_g):
    for l in range(DEPTH):
        h = _rmsnorm(x, ln1_g[l])
        p = h @ w_in[l]
        p_r, p_a, p_g = jnp.split(p, [RWKV_COLS, RWKV_COLS + ATTN_COLS], axis=-1)
        y_r = _rwkv7_mix(p_r, rwkv_mu[l], rwkv_w0[l], rwkv_decay_up[l], rwkv_a0[l],
                         rwkv_a_up[l], rwkv_g_up[l], rwkv_k_k[l], rwkv_k_a[l],
                         rwkv_r_k[l], rwkv_lnx_g[l], rwkv_lnx_b[l])
        y_a = _rmsnorm(_swa_sinks(p_a, attn_sinks[l]), attn_norm_g[l])
        y_g = _rmsnorm(_chunk_gmlp(p_g, gm_ln_g[l], gm_ln_b[l], gm_ws[l], gm_bs[l]), gm_norm_g[l])
        x = x + jnp.concatenate([y_r, y_a, y_g], axis=-1) @ w_out[l]
        h = _rmsnorm(x, ln2_g[l])
        x = x + jnp.square(jax.nn.relu(h @ w_ffn_up[l])) @ w_ffn_down[l]
    return _rmsnorm(x, lnf_g)
```

```python
import os
import contextlib
import numpy as np
import concourse.bass as bass
import concourse.mybir as mybir
from concourse.bass_utils import run_bass_kernel_spmd

F32 = mybir.dt.float32
BF16 = mybir.dt.bfloat16
AF = mybir.ActivationFunctionType
ALU = mybir.AluOpType
AX = mybir.AxisListType

NCORES = 8
L = 4
D = 2048
TT = 1024
QT = 256
NQ = TT // QT
C = 64
NCH = QT // C
NCL = 116
NDS = 24

def _groups():
    g = [np.arange(2304, 2560)]
    for i in range(6):
        a = np.arange(128 * i, 128 * i + 128)
        g.append(np.concatenate([a, 768 + a, 1536 + a]))
    for h in range(4):
        g.append(np.concatenate([2560 + np.arange(192 * h, 192 * h + 192),
                                 3328 + np.arange(64 * h, 64 * h + 64),
                                 3584 + np.arange(64 * h, 64 * h + 64)]))
    g.append(3840 + np.arange(0, 256)); g.append(3840 + np.arange(256, 512))
    g.append(4352 + np.arange(0, 256)); g.append(4352 + np.arange(256, 512))
    return g
GROUPS = _groups()
GOFF = np.concatenate([[0], np.cumsum([2048 * len(g) for g in GROUPS])]).astype(np.int64)
G_LORA, G_PAIR, G_ATTN, G_GU, G_GZ = 0, 1, 7, 11, 13


class Buf:
    __slots__ = ("w", "r")
    def __init__(self):
        self.w = []
        self.r = []


class Sched:
    def __init__(self, nc, es):
        self.nc = nc
        self.sems = []
        self.eng = {}
        for n in ("pe", "act", "dve", "pool", "sp"):
            s = es.enter_context(nc.semaphore("p_" + n))
            self.sems.append(s)
            self.eng[n] = dict(sem=len(self.sems) - 1, cnt=0, waited={}, ops=[])
        self.dsem = []
        for i in range(NDS):
            s = es.enter_context(nc.semaphore("d%d" % i))
            self.sems.append(s)
            self.dsem.append(len(self.sems) - 1)
        self.dcnt = [0] * NDS
        self.dnext = 0
        self.dnext_p = 0
        self.ccsem = len(self.sems)
        self.sems.append(es.enter_context(nc.semaphore("ccs")))
        self.cccnt = 0
        self.outstanding = []

    def _waits(self, E, reads, writes, skip_self, extra=()):
        need = {}
        for b in reads:
            for (s, v) in b.w:
                need[s] = max(need.get(s, 0), v)
        for b in writes:
            for (s, v) in b.w + b.r:
                if s == E["sem"] and not any(b is rb for rb in reads):
                    continue
                need[s] = max(need.get(s, 0), v)
        for (s, v) in extra:
            need[s] = max(need.get(s, 0), v)
        out = []
        for s, v in need.items():
            if skip_self and s == E["sem"]:
                continue
            if E["waited"].get(s, 0) < v:
                E["waited"][s] = v
                out.append((s, v))
        return out

    def _mark(self, ev, reads, writes):
        for b in writes:
            b.w = [ev]
            b.r = []
        for b in reads:
            b.r = [x for x in b.r if x[0] != ev[0]] + [ev]

    def op(self, en, fn, reads=(), writes=(), skip_self=False):
        E = self.eng[en]
        waits = self._waits(E, reads, writes, skip_self)
        sems = self.sems
        es = E["sem"]
        def run(e):
            for (s, v) in waits:
                e.wait_ge(sems[s], v)
            fn(e).then_inc(sems[es], 1)
        E["ops"].append(run)
        E["cnt"] += 1
        ev = (es, E["cnt"])
        self._mark(ev, reads, writes)
        return ev

    def dma(self, en, out, in_, reads=(), writes=()):
        E = self.eng[en]
        half = NDS // 2
        if en == "pool":
            k = half + self.dnext_p
            self.dnext_p = (self.dnext_p + 1) % half
        else:
            k = self.dnext
            self.dnext = (self.dnext + 1) % half
        extra = [(self.dsem[k], self.dcnt[k])] if self.dcnt[k] else []
        waits = self._waits(E, reads, writes, False, extra)
        sems = self.sems
        ds = self.dsem[k]
        def run(e):
            for (s, v) in waits:
                e.wait_ge(sems[s], v)
            e.dma_start(out=out, in_=in_).then_inc(sems[ds], 16)
        E["ops"].append(run)
        self.dcnt[k] += 16
        ev = (ds, self.dcnt[k])
        self._mark(ev, reads, writes)
        self.outstanding.append(ev)
        return ev

    def collective(self, in_t, out_t, reads=(), writes=()):
        E = self.eng["pool"]
        waits = self._waits(E, reads, writes, False)
        sems = self.sems
        cs = self.ccsem
        def run(e):
            for (s, v) in waits:
                e.wait_ge(sems[s], v)
            e.collective_compute("AllGather", ALU.bypass, replica_groups=[list(range(NCORES))],
                                 ins=[in_t.ap().opt()], outs=[out_t.ap().opt()]).then_inc(sems[cs], 1)
        E["ops"].append(run)
        self.cccnt += 1
        ev = (cs, self.cccnt)
        self._mark(ev, reads, writes)
        self.outstanding.append(ev)
        return ev

    def barrier(self):
        evs = [(E["sem"], E["cnt"]) for E in self.eng.values() if E["cnt"]]
        last = {}
        for (s, v) in self.outstanding:
            last[s] = max(last.get(s, 0), v)
        evs += list(last.items())
        self.outstanding = []
        sems = self.sems
        for n, E in self.eng.items():
            ws = []
            for (s, v) in evs:
                if s == E["sem"]:
                    continue
                if E["waited"].get(s, 0) < v:
                    E["waited"][s] = v
                    ws.append((s, v))
            if ws:
                def run(e, ws=ws):
                    for (s, v) in ws:
                        e.wait_ge(sems[s], v)
                E["ops"].append(run)

    def emit(self):
        nc = self.nc
        with nc.Block() as block:
            @block.tensor
            def _(e):
                for f in self.eng["pe"]["ops"]:
                    f(e)
            @block.scalar
            def _(e):
                for f in self.eng["act"]["ops"]:
                    f(e)
            @block.vector
            def _(e):
                for f in self.eng["dve"]["ops"]:
                    f(e)
            @block.gpsimd
            def _(e):
                for f in self.eng["pool"]["ops"]:
                    f(e)
            @block.sync
            def _(e):
                for f in self.eng["sp"]["ops"]:
                    f(e)


def build(depth=L, dbg=None):
    nc = bass.Bass("TRN2", target_bir_lowering=False)
    es = contextlib.ExitStack()
    S = Sched(nc, es)

    def dram_in(name, shape):
        return nc.dram_tensor(name, list(shape), F32, kind="ExternalInput")
    xT_d = dram_in("xT", [D, TT])
    WSH = {"in": (D * 4864 // 2048, 2048), "out": (16 * 128, 2048), "up": (64 * 128, 2048), "dn": (128 * 128, 1024)}
    wsh_d = {k: dram_in("wsh_" + k, [depth, v[0] // NCORES, v[1]]) for k, v in WSH.items()}
    wbn = {k: [nc.dram_tensor("wbn_%s%d" % (k, l), [v[0] // NCORES, v[1]], F32) for l in range(depth)] for k, v in WSH.items()}
    wga = {k: [nc.dram_tensor("wga_%s%d" % (k, l), [v[0], v[1]], F32) for l in range(depth)] for k, v in WSH.items()}
    b_wga = {k: [Buf() for l in range(depth)] for k in WSH}
    pcol_d = dram_in("pcol", [128, L * NCL + 16])
    pbc_d = dram_in("pbc", [L, 128, 1536])
    lora_d = dram_in("lora", [L, 128, 768])
    gup_d = dram_in("gup", [L, 128, 768])
    gmw_d = dram_in("gmw", [L, 128, 512])
    consts_d = dram_in("consts", [128, 1088])
    sel_d = dram_in("sel", [128, 17])
    out_d = nc.dram_tensor("outT", [D, TT], F32, kind="ExternalOutput")
    hsend = nc.dram_tensor("hsend", [128, 2048], F32)
    hgath = nc.dram_tensor("hgath", [NCORES * 128, 2048], F32)
    rsend = nc.dram_tensor("rsend", [64, 1536], F32)
    rgath = nc.dram_tensor("rgath", [NCORES * 64, 1536], F32)
    y0_s = nc.dram_tensor("y0_s", [12, 64, 16 * 64], F32)
    pr_s = nc.dram_tensor("pr_s", [12, 64, 16 * 64], F32)
    bon_s = nc.dram_tensor("bon_s", [12, 64, TT], F32)
    gat_s = nc.dram_tensor("gat_s", [12, 64, TT], F32)
    b_hsend, b_hgath, b_rsend, b_rgath = Buf(), Buf(), Buf(), Buf()
    b_y0 = [Buf() for _ in range(12)]; b_pr = [Buf() for _ in range(12)]
    b_bon = [Buf() for _ in range(12)]; b_gat = [Buf() for _ in range(12)]

    uid = [0]
    def sb(name, shape, dt=F32, stack=es):
        uid[0] += 1
        return stack.enter_context(nc.sbuf_tensor("s_%s_%d" % (name, uid[0]), list(shape), dt))

    xT = sb("xT", [128, 16, TT]); b_x = [Buf() for _ in range(16)]
    mixT = sb("mixT", [128, 16, TT], BF16); b_mix = [Buf() for _ in range(16)]
    cst = sb("cst", [128, 1088]); b_cst = Buf()
    pcol = sb("pcol", [128, L * NCL + 16]); b_pcol = Buf()
    selt = sb("selt", [128, 17]); b_sel = Buf()
    lora = sb("lora", [128, 768]); b_lora = Buf()
    gup = sb("gup", [128, 768]); b_gup = Buf()
    wst = [sb("wst%d" % i, [128, 16, 384], BF16) for i in range(2)]
    b_wst = [Buf(), Buf()]
    wst_i = [0]
    ps = [es.enter_context(nc.psum_tensor("ps%d" % i, [128, 512], F32)) for i in range(8)]
    b_ps = [Buf() for _ in range(8)]

    ident = cst[:, 0:128]; blockones = cst[:, 128:256]; maskM = cst[:, 256:384]
    maskNT = cst[0:64, 384:448]; tri_incl = cst[:, 448:576]; tri_prev = cst[:, 576:704]
    resetm = cst[:, 704:960]; ones = cst[:, 960:1088]

    V = lambda fn, r=(), w=(): S.op("dve", fn, r, w)
    A = lambda fn, r=(), w=(): S.op("act", fn, r, w)
    P = lambda fn, r=(), w=(): S.op("pe", fn, r, w, skip_self=True)

    S.dma("sp", cst[:], consts_d[:, :], (), (b_cst,))
    S.dma("sp", pcol[:], pcol_d[:, :], (), (b_pcol,))
    S.dma("sp", selt[:], sel_d[:, :], (), (b_sel,))
    xv = xT_d.ap().rearrange("(k p) t -> p k t", p=128)
    for kc in range(16):
        S.dma("sp", xT[:, kc, :], xv[:, kc, :], (), (b_x[kc],))

    for l in range(depth):
        for k in ("in", "out", "up", "dn"):
            bb = Buf()
            S.dma("sp", wbn[k][l].ap(), wsh_d[k].ap()[l], (), (bb,))
            S.collective(wbn[k][l], wga[k][l], (bb,), (b_wga[k][l],))

    def load_w(src_ap, ncols, nk=16, rb=None):
        i = wst_i[0]; wst_i[0] ^= 1
        S.dma("pool", wst[i][:, 0:nk, 0:ncols], src_ap, (rb,), (b_wst[i],))
        return wst[i], b_wst[i]

    def win_group(l, g):
        n = len(GROUPS[g])
        a = wga["in"][l].ap().rearrange("a b -> (a b)")[int(GOFF[g]):int(GOFF[g + 1])].rearrange("(p k c) -> p k c", p=128, k=16)
        return load_w(a, n, 16, b_wga["in"][l])

    def wtile(k, l, i, nk):
        return wga[k][l].ap().rearrange("(i p) f -> i p f", p=128)[i].rearrange("p (k c) -> p k c", k=nk)

    def norm_tmps(stk, tag):
        return ([sb(tag + "sq%d" % i, [128, 512], F32, stk) for i in range(2)], [Buf(), Buf()], sb(tag + "rs", [128, 512], F32, stk), Buf())

    def rmsnorm(tm, src, rb, n, gc0, dst, wb):
        sq, bsq, rs, brs = tm
        for kc in range(16):
            j = kc % 2
            A(lambda e, s_=src(kc), j=j: e.activation(out=sq[j][:, 0:n], in_=s_, func=AF.Square), (rb(kc),), (bsq[j],))
            P(lambda e, kc=kc, j=j: e.matmul(ps[2][:, 0:n], lhsT=ones, rhs=sq[j][:, 0:n], start=(kc == 0), stop=(kc == 15)),
              (bsq[j], b_cst), (b_ps[2],))
        A(lambda e: e.activation(out=rs[:, 0:n], in_=ps[2][:, 0:n], func=AF.Ln, scale=1.0 / D, bias=1e-6), (b_ps[2],), (brs,))
        A(lambda e: e.activation(out=rs[:, 0:n], in_=rs[:, 0:n], func=AF.Exp, scale=-0.5), (brs,), (brs,))
        for kc in range(16):
            V(lambda e, kc=kc, s_=src(kc), d_=dst(kc): e.scalar_tensor_tensor(out=d_, in0=s_, scalar=pcol[:, gc0 + kc:gc0 + kc + 1], in1=rs[:, 0:n],
                                                      op0=ALU.mult, op1=ALU.mult), (rb(kc), brs, b_pcol), (wb(kc),))

    def mm(out, lhsT, rhs, r, w, start=True, stop=True):
        P(lambda e: e.matmul(out, lhsT=lhsT, rhs=rhs, start=start, stop=stop), r, w)
    def tr(out, in_, idn, r, w):
        P(lambda e: e.matmul(out, lhsT=in_, rhs=idn, start=True, stop=True), r, w)
    def vtt(out, in0, in1, op, r, w):
        V(lambda e: e.tensor_tensor(out=out, in0=in0, in1=in1, op=op), r, w)
    def vsts(out, in0, scalar, in1, op0, op1, r, w):
        V(lambda e: e.scalar_tensor_tensor(out=out, in0=in0, scalar=scalar, in1=in1, op0=op0, op1=op1), r, w)
    def vts(out, in0, s1, s2, op0, op1, r, w):
        if s2 is None:
            V(lambda e: e.tensor_scalar(out=out, in0=in0, scalar1=s1, scalar2=None, op0=op0), r, w)
        else:
            V(lambda e: e.tensor_scalar(out=out, in0=in0, scalar1=s1, scalar2=s2, op0=op0, op1=op1), r, w)
    def vcp(out, in_, r, w):
        V(lambda e: e.tensor_copy(out=out, in_=in_), r, w)
    def vms(out, val, w):
        V(lambda e: e.memset(out, val), (), w)
    def act(out, in_, func, r, w, scale=1.0, bias=0.0, accum=None):
        if accum is None:
            A(lambda e: e.activation(out=out, in_=in_, func=func, bias=bias, scale=scale), r, w)
        else:
            A(lambda e: e.activation(out=out, in_=in_, func=func, bias=bias, scale=scale, accum_out=accum), r, w)
    def bc(ap, shape, axis=1):
        return ap.unsqueeze(axis).broadcast_to(list(shape))
    def rsqrt_act(out, in_, r, w, scale, eps):
        act(out, in_, AF.Ln, r, w, scale=scale, bias=eps)
        act(out, out, AF.Exp, w, w, scale=-0.5)

    id64 = cst[0:64, 0:64]

    def mixers(l, cols):
        (c_ln1, c_ln2, c_mu, c_w0, c_a0, c_kk, c_ka, c_rk, c_lxg, c_lxb, c_ag, c_gg, c_sk) = cols
        xall = tuple(b_x)
        with contextlib.ExitStack() as lst:
            hq = sb("hq", [128, 16, 384], BF16, lst); b_hq = [Buf() for _ in range(16)]
            tm = norm_tmps(lst, "n1")
            STall = sb("STall", [64, 12, 128], F32, lst); b_ST = [Buf() for _ in range(12)]
            hp0 = sb("hp0", [64, 12, 3], F32, lst); b_hp0 = Buf()
            es_ = sb("es", [128, 12], F32, lst); b_es = Buf()
            wtm = None; b_wtm = None
            Ssel_t = None
            S.dma("sp", hsend.ap().rearrange("p (k t) -> p k t", k=16), xT[:, :, TT - 128:TT], xall, (b_hsend,))
            S.collective(hsend, hgath, (b_hsend,), (b_hgath,))
            hgv = hgath.ap().rearrange("(r p) f -> p r f", p=128)
            with contextlib.ExitStack() as st0:
                xh = sb("xh", [128, 2048], F32, st0); b_xh = Buf()
                gtmp = [sb("gtmp%d" % i, [128, 8, 256], F32, st0) for i in range(2)]; b_gt = [Buf(), Buf()]
                for j in range(8):
                    g = gtmp[j % 2]; bg = b_gt[j % 2]
                    S.dma("sp", g[:], hgv[:, :, j * 256:(j + 1) * 256], (b_hgath,), (bg,))
                    xv_ = xh[:, j * 256:(j + 1) * 256]
                    vts(xv_, g[:, 0, :], selt[:, 0:1], None, ALU.mult, None, (bg, b_sel), (b_xh,))
                    for r in range(1, 8):
                        vsts(xv_, g[:, r, :], selt[:, r:r + 1], xv_, ALU.mult, ALU.add, (bg, b_sel, b_xh), (b_xh,))
                for j in range(12):
                    pb_ = (j % 2) * 64
                    vcp(hp0[:, j, 0:1], pcol[pb_:pb_ + 64, c_lxg + j // 2:c_lxg + j // 2 + 1], (b_pcol,), (b_hp0,))
                    vcp(hp0[:, j, 1:2], pcol[pb_:pb_ + 64, c_lxb + j // 2:c_lxb + j // 2 + 1], (b_pcol,), (b_hp0,))
                    vcp(hp0[:, j, 2:3], pcol[pb_:pb_ + 64, c_ag + j // 2:c_ag + j // 2 + 1], (b_pcol,), (b_hp0,))
                act(es_[:, :], pcol[:, c_sk:c_sk + 12], AF.Exp, (b_pcol,), (b_es,))
                for H in range(12):
                    vms(STall[:, H, 0:64], 0.0, (b_ST[H],))
                    vcp(STall[:, H, 64:128], id64, (b_cst,), (b_ST[H],))
                rmsnorm(tm, lambda kc: xh[:, kc * 128:(kc + 1) * 128], lambda kc: b_xh, 128, c_ln1, lambda kc: hq[:, kc, 0:128], lambda kc: b_hq[kc])
                S.barrier()
            for q in range(NQ):
                t0 = q * QT
                if q > 0:
                    rmsnorm(tm, lambda kc: xT[:, kc, t0 - 128:t0], lambda kc: b_x[kc], 128, c_ln1, lambda kc: hq[:, kc, 0:128], lambda kc: b_hq[kc])
                rmsnorm(tm, lambda kc: xT[:, kc, t0:t0 + QT], lambda kc: b_x[kc], QT, c_ln1, lambda kc: hq[:, kc, 128:384], lambda kc: b_hq[kc])
                hall = tuple(b_hq)
                if not (dbg and "norwkv" in dbg):
                    rwkv_quarter(l, q, t0, cols, hq, hall, STall, b_ST)
                    S.barrier()
                if not (dbg and "noswa" in dbg):
                    swa_quarter(l, q, t0, cols, hq, hall, hp0, b_hp0, es_, b_es, tm)
                    S.barrier()
                if not (dbg and "nogm" in dbg):
                    gmlp_quarter(l, q, t0, cols, hq, hall, wtm, b_wtm, tm)
                    S.barrier()
            if not (dbg and ("norwkv" in dbg or "rw" in dbg)):
                rwkv_post(l, cols, STall, b_ST, hp0, b_hp0, Ssel_t)
            S.barrier()

    def proj_fm(psb, w, bw, c0, c1, hq, hall, n0, n1):
        for kc in range(16):
            mm(ps[psb][0:c1 - c0, 0:n1 - n0], w[:, kc, c0:c1], hq[:, kc, n0:n1], (bw, hall[kc]), (b_ps[psb],), start=(kc == 0), stop=(kc == 15))

    def gmlp_quarter(l, q, t0, cols, hq, hall, wtm, b_wtm, tm):
        c_gg = cols[11]
        with contextlib.ExitStack() as st:
            u_sb = sb("u_sb", [128, 4, QT], F32, st); b_u = Buf()
            og = sb("og", [128, 4, QT], F32, st); b_og = Buf()
            zg = sb("zg", [128, 512], F32, st); b_zg = Buf()
            z2 = sb("z2", [128, 512], F32, st); b_z2 = Buf()
            sm = sb("gsm", [128, 4], F32, st); b_sm = Buf()
            wtm = sb("wtm", [128, 4, 128], F32, st); b_wtm = Buf()
            pbc = sb("pbc", [128, 1536], F32, st); b_pbc = Buf()
            gmw = sb("gmw", [128, 512], F32, st); b_gmw = Buf()
            S.dma("sp", pbc[:], pbc_d.ap()[l], (), (b_pbc,))
            S.dma("sp", gmw[:], gmw_d.ap()[l], (), (b_gmw,))
            vtt(wtm[:, :, :], gmw[:, :].rearrange("p (h t) -> p h t", h=4), bc(tri_incl, [128, 4, 128]), ALU.mult, (b_gmw, b_cst), (b_wtm,))
            for uh in range(2):
                w, bw = win_group(l, G_GU + uh)
                for hh in range(2):
                    h = 2 * uh + hh
                    pb = hh
                    proj_fm(pb, w, bw, hh * 128, hh * 128 + 128, hq, hall, 128, 384)
                    act(u_sb[:, h, :], ps[pb][:, 0:QT], AF.Gelu, (b_ps[pb],), (b_u,))
            zb = (3, 4)
            for zh in range(2):
                w, bw = win_group(l, G_GZ + zh)
                for tb in range(2):
                    for kc in range(16):
                        mm(ps[zb[tb]][:, zh * 256:(zh + 1) * 256], hq[:, kc, 128 + tb * 128:256 + tb * 128], w[:, kc, 0:256], (bw, hall[kc]), (b_ps[zb[tb]],),
                           start=(kc == 0), stop=(kc == 15))
            for tb in range(2):
                zp = ps[zb[tb]]
                act(zg[:, :], zp[:, :], AF.Gelu, (b_ps[zb[tb]],), (b_zg,))
                V(lambda e: e.reduce_sum(out=sm[:, 0:1], in_=zg[:, :], axis=AX.X), (b_zg,), (b_sm,))
                vts(sm[:, 1:2], sm[:, 0:1], 1.0 / 512, None, ALU.mult, None, (b_sm,), (b_sm,))
                vts(zg[:, :], zg[:, :], sm[:, 1:2], None, ALU.subtract, None, (b_zg, b_sm), (b_zg,))
                act(z2[:, :], zg[:, :], AF.Square, (b_zg,), (b_z2, b_sm), accum=sm[:, 2:3])
                rsqrt_act(sm[:, 3:4], sm[:, 2:3], (b_sm,), (b_sm,), 1.0 / 512, 1e-6)
                vsts(z2[:, :], zg[:, :], sm[:, 3:4], pbc[:, 0:512], ALU.mult, ALU.mult, (b_zg, b_sm, b_pbc), (b_z2,))
                vtt(z2[:, :], z2[:, :], pbc[:, 512:1024], ALU.add, (b_z2, b_pbc), (b_z2,))
                for h in range(4):
                    mm(ps[5][:, h * 128:(h + 1) * 128], z2[:, h * 128:(h + 1) * 128], wtm[:, h, :], (b_z2, b_wtm), (b_ps[5],))
                ogv = og[:, :, tb * 128:(tb + 1) * 128]
                vtt(ogv, ps[5][:, :].rearrange("p (h t) -> p h t", h=4), pbc[:, 1024:1536].rearrange("p (h t) -> p h t", h=4), ALU.add, (b_ps[5], b_pbc), (b_og,))
                vtt(ogv, ogv, u_sb[:, :, tb * 128:(tb + 1) * 128], ALU.mult, (b_og, b_u), (b_og,))
            sq, bsq, rs, brs = tm
            for h in range(4):
                j = h % 2
                act(sq[j][:, 0:QT], og[:, h, :], AF.Square, (b_og,), (bsq[j],))
                mm(ps[2][:, 0:QT], ones, sq[j][:, 0:QT], (bsq[j], b_cst), (b_ps[2],), start=(h == 0), stop=(h == 3))
            rsqrt_act(rs[:, 0:QT], ps[2][:, 0:QT], (b_ps[2],), (brs,), 1.0 / 512, 1e-6)
            for h in range(4):
                vsts(mixT[:, 12 + h, t0:t0 + QT], og[:, h, :], pcol[:, c_gg + h:c_gg + h + 1], rs[:, 0:QT], ALU.mult, ALU.mult, (b_og, brs, b_pcol), (b_mix[12 + h],))

    def swa_quarter(l, q, t0, cols, hq, hall, hp0, b_hp0, es_, b_es, tm):
        with contextlib.ExitStack() as st:
            qs = sb("qs", [64, 2, 3, 128], F32, st); b_qs = Buf()
            kTs = sb("kTs", [64, 384], F32, st); b_kT = Buf()
            vs_ = sb("vs", [128, 3, 64], F32, st); b_vs = Buf()
            pTp = sb("pTp", [128, 384], F32, st); b_pp = Buf()
            pTc = sb("pTc", [128, 384], F32, st); b_pc = Buf()
            den = sb("den", [64, 384], F32, st); b_den = Buf()
            oall = sb("oall", [64, 12, QT], F32, st); b_oa = Buf()
            for h in range(4):
                w, bw = win_group(l, G_ATTN + h)
                proj_fm(0, w, bw, 0, 128, hq, hall, 128, 384)
                proj_fm(1, w, bw, 128, 192, hq, hall, 128, 384)
                proj_fm(3, w, bw, 192, 256, hq, hall, 0, 384)
                for blk in range(3):
                    for kc in range(16):
                        mm(ps[4][:, blk * 64:(blk + 1) * 64], hq[:, kc, blk * 128:(blk + 1) * 128], w[:, kc, 256:320], (bw, hall[kc]), (b_ps[4],),
                           start=(kc == 0), stop=(kc == 15))
                srcs = [(ps[0][0:64, 0:QT], 0), (ps[0][64:128, 0:QT], 0), (ps[1][0:64, 0:QT], 1)]
                for g in range(3):
                    sp_, bi = srcs[g]
                    act(qs[:, :, g, :], sp_.rearrange("p (b t) -> p b t", b=2), AF.Copy, (b_ps[bi],), (b_qs,), scale=0.125)
                vcp(kTs[:, :], ps[3][0:64, 0:384], (b_ps[3],), (b_kT,))
                vcp(vs_[:, :, :], ps[4][:, 0:192].rearrange("p (b d) -> p b d", b=3), (b_ps[4],), (b_vs,))
                if dbg and "swap1" in dbg:
                    continue
                for b in range(2):
                    qv = qs[:, b, :, :].rearrange("p g t -> p (g t)")
                    mm(ps[5][:, 0:384], kTs[:, b * 128:(b + 1) * 128], qv, (b_kT, b_qs), (b_ps[5],))
                    mm(ps[6][:, 0:384], kTs[:, (b + 1) * 128:(b + 2) * 128], qv, (b_kT, b_qs), (b_ps[6],))
                    act(pTp[:, :], ps[5][:, 0:384], AF.Exp, (b_ps[5],), (b_pp,))
                    act(pTc[:, :], ps[6][:, 0:384], AF.Exp, (b_ps[6],), (b_pc,))
                    pv3 = pTp[:, :].rearrange("p (g t) -> p g t", g=3)
                    if q == 0 and b == 0:
                        vsts(pv3, pv3, selt[:, 8:9], bc(tri_prev, [128, 3, 128]), ALU.mult, ALU.mult, (b_pp, b_sel, b_cst), (b_pp,))
                    else:
                        vtt(pv3, pv3, bc(tri_prev, [128, 3, 128]), ALU.mult, (b_pp, b_cst), (b_pp,))
                    cv3 = pTc[:, :].rearrange("p (g t) -> p g t", g=3)
                    vtt(cv3, cv3, bc(tri_incl, [128, 3, 128]), ALU.mult, (b_pc, b_cst), (b_pc,))
                    if dbg and "swap2" in dbg:
                        continue
                    mm(ps[7][0:64, 0:384], vs_[:, b, :], pTp[:, :], (b_vs, b_pp), (b_ps[7],), start=True, stop=False)
                    mm(ps[7][0:64, 0:384], vs_[:, b + 1, :], pTc[:, :], (b_vs, b_pc), (b_ps[7],), start=False, stop=True)
                    mm(ps[0][0:64, 0:384], ones[:, 0:64], pTp[:, :], (b_cst, b_pp), (b_ps[0],), start=True, stop=False)
                    mm(ps[0][0:64, 0:384], ones[:, 0:64], pTc[:, :], (b_cst, b_pc), (b_ps[0],), start=False, stop=True)
                    for g in range(3):
                        vts(den[:, g * 128:(g + 1) * 128], ps[0][0:64, g * 128:(g + 1) * 128], es_[0:64, 3 * h + g:3 * h + g + 1], None, ALU.add, None,
                            (b_ps[0], b_es), (b_den,))
                    V(lambda e: e.reciprocal(out=den[:, :], in_=den[:, :]), (b_den,), (b_den,))
                    vtt(oall[:, 3 * h:3 * h + 3, b * 128:(b + 1) * 128], ps[7][0:64, 0:384].rearrange("p (g t) -> p g t", g=3),
                        den[:, :].rearrange("p (g t) -> p g t", g=3), ALU.mult, (b_ps[7], b_den), (b_oa,))
            sq, bsq, rs, brs = tm
            if dbg and "swap" in dbg:
                return
            for j in range(12):
                k = j % 2
                act(sq[k][0:64, 0:QT], oall[:, j, :], AF.Square, (b_oa,), (bsq[k],))
                mm(ps[2][:, 0:QT], ones[0:64, :], sq[k][0:64, 0:QT], (bsq[k], b_cst), (b_ps[2],), start=(j == 0), stop=(j == 11))
            rsqrt_act(rs[:, 0:QT], ps[2][:, 0:QT], (b_ps[2],), (brs,), 1.0 / 768, 1e-6)
            for j in range(12):
                pb_ = (j % 2) * 64
                vsts(mixT[pb_:pb_ + 64, 6 + j // 2, t0:t0 + QT], oall[:, j, :], hp0[:, j, 2:3], rs[0:64, 0:QT], ALU.mult, ALU.mult,
                     (b_oa, brs, b_hp0), (b_mix[6 + j // 2],))

    NEG_E = -0.6065306597126334

    def rwkv_quarter(l, q, t0, cols, hq, hall, STall, b_ST):
        (c_ln1, c_ln2, c_mu, c_w0, c_a0, c_kk, c_ka, c_rk, c_lxg, c_lxb, c_ag, c_gg, c_sk) = cols
        with contextlib.ExitStack() as st:
            T2 = lambda name, shape: (sb(name, shape, F32, st), Buf())
            lsh, b_lsh = T2("lsh", [128, 2, QT])
            praw, b_praw = T2("praw", [128, 3, QT + 1]); sh, b_sh = T2("sh", [128, 3, QT])
            lpr, b_lpr = praw, b_praw
            lag, b_lag = T2("lag", [128, 3, QT])
            kk, b_kk = T2("kk", [128, QT]); kp, b_kp = T2("kp", [128, QT]); Gt, b_G = T2("Gt", [128, QT])
            e1, b_e1 = T2("e1", [128, QT]); e2, b_e2 = T2("e2", [128, QT]); e3, b_e3 = T2("e3", [128, QT]); e4, b_e4 = T2("e4", [128, QT])
            bb, b_bb = T2("bb", [128, QT]); tmp, b_tmp = T2("tmp", [128, QT]); bon, b_bon_t = e4, b_e4
            AR, b_AR = T2("AR", [64, NCH, 128]); DT, b_DT = T2("DT", [64, NCH, 128]); DHf, b_DHf = T2("DHf", [64, NCH, 128])
            VV, b_VV = T2("VV", [64, NCH, 128]); gCh, b_gCh = T2("gCh", [64, NCH])
            Mm, b_Mm = T2("Mm", [128, NCH, 128]); NT0, b_NT0 = T2("NT0", [64, NCH, 64])
            NRB = int(os.environ.get("K_NRB", 5))
            Nb_ = [T2("Nb%d" % i, [64, NCH, 64]) for i in range(NRB)]; NTb_ = [T2("NTb%d" % i, [64, NCH, 64]) for i in range(NRB)]
            Tm, b_Tm = T2("Tm", [64, NCH, 64])
            vms(VV[:, :, 0:64], 0.0, (b_VV,))
            if not (dbg and "rw2" in dbg):
                DHtm, b_DHtm = T2("DHtm", [128, NCH, 64]); Z, b_Z = T2("Z", [128, NCH, 128])
                Xs, b_Xs = T2("Xs", [64, 128]); y0sb, b_y0sb = T2("y0sb", [64, NCH * 64]); prsb, b_prsb = T2("prsb", [64, NCH * 64])
                vms(Z[64:128, :, 64:128], 0.0, (b_Z,))

            def shifted(raw, braw, j, dst, bdst, mucol, psb, w, bw, c0, c1):
                nrow = c1 - c0
                proj_fm(psb, w, bw, c0, c1, hq, hall, 128, 384)
                proj_fm(7, w, bw, c0, c1, hq, hall, 127, 128)
                act(raw[0:nrow, j, 1:QT + 1], ps[psb][0:nrow, 0:QT], AF.Copy, (b_ps[psb],), (braw,))
                vcp(raw[0:nrow, j, 0:1], ps[7][0:nrow, 0:1], (b_ps[7],), (braw,))
                vtt(dst[0:nrow, j, :], raw[0:nrow, j, 0:QT], raw[0:nrow, j, 1:QT + 1], ALU.subtract, (braw,), (bdst,))
                vsts(dst[0:nrow, j, :], dst[0:nrow, j, :], pcol[0:nrow, mucol:mucol + 1], raw[0:nrow, j, 1:QT + 1], ALU.mult, ALU.add, (bdst, braw, b_pcol), (bdst,))

            w, bw = win_group(l, G_LORA)
            shifted(lpr, b_lpr, 0, lsh, b_lsh, c_mu + 18, 0, w, bw, 0, 128)
            shifted(lpr, b_lpr, 1, lsh, b_lsh, c_mu + 19, 1, w, bw, 128, 256)
            act(lsh[0:64, 0, :], lsh[0:64, 0, :], AF.Tanh, (b_lsh,), (b_lsh,))
            act(lsh[:, 1, :], lsh[:, 1, :], AF.Sigmoid, (b_lsh,), (b_lsh,))
            b2d = bon_s.ap().rearrange("h i t -> (h i) t"); g2d = gat_s.ap().rearrange("h i t -> (h i) t")
            for i in range(6):
                w, bw = win_group(l, G_PAIR + i)
                for j in range(3):
                    shifted(praw, b_praw, j, sh, b_sh, c_mu + 3 * i + j, j % 2, w, bw, j * 128, j * 128 + 128)
                rs_, ks_, vs_ = sh[:, 0, :], sh[:, 1, :], sh[:, 2, :]
                mm(ps[3][:, 0:QT], lora[0:64, i * 128:(i + 1) * 128], lsh[0:64, 0, :], (b_lora, b_lsh), (b_ps[3],))
                act(lag[:, 0, :], ps[3][:, 0:QT], AF.Sigmoid, (b_ps[3], b_pcol), (b_lag,), bias=pcol[:, c_w0 + i:c_w0 + i + 1])
                vts(lag[:, 0, :], lag[:, 0, :], NEG_E, None, ALU.mult, None, (b_lag,), (b_lag,))
                mm(ps[4][:, 0:QT], lora[64:128, i * 128:(i + 1) * 128], lsh[64:128, 0, :], (b_lora, b_lsh), (b_ps[4],))
                act(lag[:, 1, :], ps[4][:, 0:QT], AF.Sigmoid, (b_ps[4], b_pcol), (b_lag,), bias=pcol[:, c_a0 + i:c_a0 + i + 1])
                mm(ps[5][:, 0:QT], gup[:, i * 128:(i + 1) * 128], lsh[:, 1, :], (b_gup, b_lsh), (b_ps[5],))
                act(lag[:, 2, :], ps[5][:, 0:QT], AF.Copy, (b_ps[5],), (b_lag,))
                lw_, a_ = lag[:, 0, :], lag[:, 1, :]
                S.dma("sp", g2d[128 * i:128 * i + 128, t0:t0 + QT], lag[:, 2, :], (b_lag,), (b_gat[2 * i], b_gat[2 * i + 1]))
                vts(kk[:, :], ks_, pcol[:, c_kk + i:c_kk + i + 1], None, ALU.mult, None, (b_sh, b_pcol), (b_kk,))
                vtt(tmp[:, :], kk[:, :], kk[:, :], ALU.mult, (b_kk,), (b_tmp,))
                mm(ps[6][:, 0:QT], blockones, tmp[:, :], (b_cst, b_tmp), (b_ps[6],))
                rsqrt_act(e1[:, :], ps[6][:, 0:QT], (b_ps[6],), (b_e1,), 1.0, 1e-20)
                vtt(kk[:, :], kk[:, :], e1[:, :], ALU.mult, (b_kk, b_e1), (b_kk,))
                vts(tmp[:, :], a_, -1.0, pcol[:, c_ka + i:c_ka + i + 1], ALU.add, ALU.mult, (b_lag, b_pcol), (b_tmp,))
                vsts(kp[:, :], tmp[:, :], 1.0, ks_, ALU.add, ALU.mult, (b_tmp, b_sh), (b_kp,))
                vsts(tmp[:, :], rs_, pcol[:, c_rk + i:c_rk + i + 1], kp[:, :], ALU.mult, ALU.mult, (b_sh, b_pcol, b_kp), (b_tmp,))
                mm(ps[6][:, 0:QT], blockones, tmp[:, :], (b_cst, b_tmp), (b_ps[6],))
                vtt(bon[:, :], ps[6][:, 0:QT], vs_, ALU.mult, (b_ps[6], b_sh), (b_bon_t,))
                S.dma("sp", b2d[128 * i:128 * i + 128, t0:t0 + QT], bon[:, :], (b_bon_t,), (b_bon[2 * i], b_bon[2 * i + 1]))
                V(lambda e: e.tensor_tensor_scan(out=Gt[:, :], data0=resetm, data1=lw_, initial=0.0, op0=ALU.mult, op1=ALU.add), (b_cst, b_lag), (b_G,))
                act(e1[:, :], Gt[:, :], AF.Exp, (b_G,), (b_e1,))
                act(e2[:, :], Gt[:, :], AF.Exp, (b_G,), (b_e2,), scale=-1.0)
                vtt(tmp[:, :], Gt[:, :], lw_, ALU.subtract, (b_G, b_lag), (b_tmp,))
                act(e3[:, :], tmp[:, :], AF.Exp, (b_tmp,), (b_e3,))
                G3 = Gt[:, :].rearrange("p (c t) -> p c t", c=NCH)
                vtt(tmp[:, :].rearrange("p (c t) -> p c t", c=NCH), G3[:, :, 63:64].broadcast_to([128, NCH, 64]), G3, ALU.subtract, (b_G,), (b_tmp,))
                act(e4[:, :], tmp[:, :], AF.Exp, (b_tmp,), (b_e4,))
                vtt(bb[:, :], kk[:, :], a_, ALU.mult, (b_kk, b_lag), (b_bb,))
                for e_ in range(2):
                    H = 2 * i + e_
                    r0, r1 = e_ * 64, e_ * 64 + 64
                    c3 = lambda ap: ap.rearrange("p (c t) -> p c t", c=NCH)
                    vsts(AR[:, :, 0:64], c3(kk[r0:r1, :]), -1.0, c3(e3[r0:r1, :]), ALU.mult, ALU.mult, (b_kk, b_e3), (b_AR,))
                    vtt(AR[:, :, 64:128], c3(sh[r0:r1, 0, :]), c3(e1[r0:r1, :]), ALU.mult, (b_sh, b_e1), (b_AR,))
                    vtt(DT[:, :, 0:64], c3(bb[r0:r1, :]), c3(e2[r0:r1, :]), ALU.mult, (b_bb, b_e2), (b_DT,))
                    vtt(DT[:, :, 64:128], c3(kp[r0:r1, :]), c3(e2[r0:r1, :]), ALU.mult, (b_kp, b_e2), (b_DT,))
                    vtt(DHf[:, :, 0:64], c3(bb[r0:r1, :]), c3(e4[r0:r1, :]), ALU.mult, (b_bb, b_e4), (b_DHf,))
                    vtt(DHf[:, :, 64:128], c3(kp[r0:r1, :]), c3(e4[r0:r1, :]), ALU.mult, (b_kp, b_e4), (b_DHf,))
                    vcp(VV[:, :, 64:128], c3(sh[r0:r1, 2, :]), (b_sh,), (b_VV,))
                    vcp(gCh[:, :], c3(e1[r0:r1, :])[:, :, 63], (b_e1,), (b_gCh,))
                    if dbg and "rw1" in dbg:
                        continue
                    for c in range(NCH):
                        mm(ps[0][:, c * 128:(c + 1) * 128], DT[:, c, :], AR[:, c, :], (b_DT, b_AR), (b_ps[0],))
                        mm(ps[1][0:64, c * 64:(c + 1) * 64], AR[:, c, 0:64], DT[:, c, 0:64], (b_DT, b_AR), (b_ps[1],))
                    vtt(Mm[:, :, :], ps[0][:, 0:NCH * 128].rearrange("p (c t) -> p c t", c=NCH), bc(maskM, [128, NCH, 128]), ALU.mult, (b_ps[0], b_cst), (b_Mm,))
                    vtt(NT0[:, :, :], ps[1][0:64, 0:NCH * 64].rearrange("p (c t) -> p c t", c=NCH), bc(maskNT, [64, NCH, 64]), ALU.mult, (b_ps[1], b_cst), (b_NT0,))
                    vtt(Tm[:, :, :], Mm[0:64, :, 0:64], bc(id64, [64, NCH, 64]), ALU.add, (b_Mm, b_cst), (b_Tm,))
                    if dbg and "rw2a" in dbg:
                        continue
                    Ncur = lambda c: Mm[0:64, c, 0:64]; bN = b_Mm
                    NTcur = lambda c: NT0[:, c, :]; bNT = b_NT0
                    for k in range(int(os.environ.get('K_RND', 5))):
                        Nn, bNn = Nb_[k % NRB]; NTn, bNTn = NTb_[k % NRB]
                        for c in range(NCH):
                            mm(ps[3][0:64, c * 64:(c + 1) * 64], NTcur(c), Ncur(c), (bN, bNT), (b_ps[3],))
                            mm(ps[4][0:64, c * 64:(c + 1) * 64], Ncur(c), NTcur(c), (bN, bNT), (b_ps[4],))
                        vcp(Nn[:, :, :], ps[3][0:64, 0:NCH * 64].rearrange("p (c t) -> p c t", c=NCH), (b_ps[3],), (bNn,))
                        vcp(NTn[:, :, :], ps[4][0:64, 0:NCH * 64].rearrange("p (c t) -> p c t", c=NCH), (b_ps[4],), (bNTn,))
                        for c in range(NCH):
                            mm(ps[5][0:64, c * 64:(c + 1) * 64], NTn[:, c, :], Tm[:, c, :], (bNTn, b_Tm), (b_ps[5],))
                        vtt(Tm[:, :, :], Tm[:, :, :], ps[5][0:64, 0:NCH * 64].rearrange("p (c t) -> p c t", c=NCH), ALU.add, (b_Tm, b_ps[5]), (b_Tm,))
                        Ncur = (lambda Nn: (lambda c: Nn[:, c, :]))(Nn); bN = bNn
                        NTcur = (lambda NTn: (lambda c: NTn[:, c, :]))(NTn); bNT = bNTn
                    if dbg and "rw2" in dbg:
                        continue
                    for c in range(NCH):
                        tr(ps[6][:, c * 64:(c + 1) * 64], DHf[:, c, :], id64, (b_DHf, b_cst), (b_ps[6],))
                        tr(ps[7][:, c * 64:(c + 1) * 64], VV[:, c, :], id64, (b_VV, b_cst), (b_ps[7],))
                    vcp(DHtm[:, :, :], ps[6][:, 0:NCH * 64].rearrange("p (c t) -> p c t", c=NCH), (b_ps[6],), (b_DHtm,))
                    vcp(Z[64:128, :, 0:64], ps[7][64:128, 0:NCH * 64].rearrange("p (c t) -> p c t", c=NCH), (b_ps[7],), (b_Z,))
                    ST = STall[:, H, :]; bS = b_ST[H]
                    for c in range(NCH):
                        mm(ps[0][0:64, 0:128], AR[:, c, 0:64], ST, (b_AR, bS), (b_ps[0],))
                        mm(ps[6][0:64, 0:128], Mm[64:128, c, 0:64], Z[64:128, c, :], (b_Mm, b_Z), (b_ps[6],))
                        vcp(Xs[:, :], ps[0][0:64, 0:128], (b_ps[0],), (b_Xs,))
                        vtt(Xs[:, :], Xs[:, :], ps[6][0:64, 0:128], ALU.add, (b_Xs, b_ps[6]), (b_Xs,))
                        mm(ps[1][0:64, 0:128], Tm[:, c, :], Xs[:, :], (b_Tm, b_Xs), (b_ps[1],))
                        vcp(Z[0:64, c, :], ps[1][0:64, 0:128], (b_ps[1],), (b_Z,))
                        mm(ps[3][0:64, c * 64:(c + 1) * 64], AR[:, c, 64:128], ST[:, 0:64], (b_AR, bS), (b_ps[3],), start=True, stop=False)
                        mm(ps[3][0:64, c * 64:(c + 1) * 64], Mm[:, c, 64:128], Z[:, c, 0:64], (b_Mm, b_Z), (b_ps[3],), start=False, stop=True)
                        mm(ps[4][0:64, c * 64:(c + 1) * 64], ST[:, 64:128], AR[:, c, 64:128], (b_AR, bS), (b_ps[4],), start=True, stop=False)
                        mm(ps[4][0:64, c * 64:(c + 1) * 64], Z[:, c, 64:128], Mm[:, c, 64:128], (b_Mm, b_Z), (b_ps[4],), start=False, stop=True)
                        mm(ps[5][0:64, 0:128], DHtm[:, c, :], Z[:, c, :], (b_DHtm, b_Z), (b_ps[5],))
                        vsts(ST, ST, gCh[:, c:c + 1], ps[5][0:64, 0:128], ALU.mult, ALU.add, (bS, b_gCh, b_ps[5]), (bS,))
                    vcp(y0sb[:, :], ps[3][0:64, 0:NCH * 64], (b_ps[3],), (b_y0sb,))
                    vcp(prsb[:, :], ps[4][0:64, 0:NCH * 64], (b_ps[4],), (b_prsb,))
                    S.dma("sp", y0_s.ap()[H][:, q * QT:(q + 1) * QT], y0sb[:, :], (b_y0sb,), (b_y0[H],))
                    S.dma("sp", pr_s.ap()[H][:, q * QT:(q + 1) * QT], prsb[:, :], (b_prsb,), (b_pr[H],))

    def rwkv_post(l, cols, STall, b_ST, hp0, b_hp0, Ssel_t):
        with contextlib.ExitStack() as so:
            Ssel_t = (sb("Ssel", [64, 12, 64], F32, so), Buf())
            rwkv_post_(l, cols, STall, b_ST, hp0, b_hp0, Ssel_t)
            S.barrier()

    def rwkv_post_(l, cols, STall, b_ST, hp0, b_hp0, Ssel_t):
        with contextlib.ExitStack() as st:
            T2 = lambda name, shape: (sb(name, shape, F32, st), Buf())
            sbuf_, b_sbuf = T2("sendb", [64, 12, 128])
            Rr = [T2("Rr%d" % i, [64, 12, 128]) for i in range(2)]
            Sin = [T2("Sin%d" % i, [64, 12, 64]) for i in range(2)]
            Ssel, b_Ssel = Ssel_t
            sall = tuple(b_ST)
            for H in range(12):
                bk = 0 if H < 8 else 1
                tr(ps[bk][0:64, (H % 8) * 64:(H % 8) * 64 + 64], STall[:, H, 64:128], id64, (b_ST[H], b_cst), (b_ps[bk],))
            vcp(sbuf_[:, :, 0:64], STall[:, :, 0:64], sall, (b_sbuf,))
            vcp(sbuf_[:, 0:8, 64:128], ps[0][0:64, 0:512].rearrange("p (h t) -> p h t", h=8), (b_ps[0],), (b_sbuf,))
            vcp(sbuf_[:, 8:12, 64:128], ps[1][0:64, 0:256].rearrange("p (h t) -> p h t", h=4), (b_ps[1],), (b_sbuf,))
            S.dma("sp", rsend.ap().rearrange("p (h t) -> p h t", h=12), sbuf_[:, :, :], (b_sbuf,), (b_rsend,))
            S.collective(rsend, rgath, (b_rsend,), (b_rgath,))
            rgv = rgath.ap().rearrange("(r p) (h t) -> r p h t", p=64, h=12)
            vms(Sin[0][0][:, :, :], 0.0, (Sin[0][1],))
            vms(Ssel[:, :, :], 0.0, (b_Ssel,))
            for r in range(7):
                R_, bR = Rr[r % 2]
                S.dma("sp", R_[:, :, :], rgv[r], (b_rgath,), (bR,))
                Sc, bSc = Sin[r % 2]; Sn, bSn = Sin[(r + 1) % 2]
                for H in range(12):
                    bk = 3 if H < 8 else 4
                    mm(ps[bk][0:64, (H % 8) * 64:(H % 8) * 64 + 64], R_[:, H, 64:128], Sc[:, H, :], (bR, bSc), (b_ps[bk],))
                vtt(Sn[:, 0:8, :], ps[3][0:64, 0:512].rearrange("p (h t) -> p h t", h=8), R_[:, 0:8, 0:64], ALU.add, (b_ps[3], bR), (bSn,))
                vtt(Sn[:, 8:12, :], ps[4][0:64, 0:256].rearrange("p (h t) -> p h t", h=4), R_[:, 8:12, 0:64], ALU.add, (b_ps[4], bR), (bSn,))
                vsts(Ssel[:, :, :], Sn[:, :, :], selt[0:64, 10 + r:11 + r], Ssel[:, :, :], ALU.mult, ALU.add, (bSn, b_sel, b_Ssel), (b_Ssel,))
        S.barrier()
        with contextlib.ExitStack() as st:
            T2 = lambda name, shape: (sb(name, shape, F32, st), Buf())
            y0t, b_y0t = T2("y0t", [64, 16, 64]); prt, b_prt = T2("prt", [64, 16, 64])
            yt, b_yt = T2("yt", [64, 16, 64]); ysq, b_ysq = T2("ysq", [64, 16, 64])
            stt, b_stt = T2("stt", [64, 5, 16])
            yf, b_yf = T2("yf", [64, TT]); bnt, b_bnt = T2("bnt", [64, TT]); gtt, b_gtt = T2("gtt", [64, TT])
            for H in range(12):
                S.dma("sp", y0t[:, :, :], y0_s.ap()[H].rearrange("t (c i) -> t c i", c=16), (b_y0[H],), (b_y0t,))
                S.dma("sp", prt[:, :, :], pr_s.ap()[H].rearrange("j (c t) -> j c t", c=16), (b_pr[H],), (b_prt,))
                S.dma("sp", bnt[:, :], bon_s.ap()[H], (b_bon[H],), (b_bnt,))
                S.dma("sp", gtt[:, :], gat_s.ap()[H], (b_gat[H],), (b_gtt,))
                for c in range(16):
                    bk = 5 if c < 8 else 6
                    mm(ps[bk][0:64, (c % 8) * 64:(c % 8) * 64 + 64], prt[:, c, :], Ssel[:, H, :], (b_prt, b_Ssel), (b_ps[bk],))
                vtt(yt[:, 0:8, :], ps[5][0:64, 0:512].rearrange("p (c i) -> p c i", c=8), y0t[:, 0:8, :], ALU.add, (b_ps[5], b_y0t), (b_yt,))
                vtt(yt[:, 8:16, :], ps[6][0:64, 0:512].rearrange("p (c i) -> p c i", c=8), y0t[:, 8:16, :], ALU.add, (b_ps[6], b_y0t), (b_yt,))
                V(lambda e: e.reduce_sum(out=stt[:, 0, :], in_=yt[:, :, :], axis=AX.X), (b_yt,), (b_stt,))
                vtt(ysq[:, :, :], yt[:, :, :], yt[:, :, :], ALU.mult, (b_yt,), (b_ysq,))
                V(lambda e: e.reduce_sum(out=stt[:, 1, :], in_=ysq[:, :, :], axis=AX.X), (b_ysq,), (b_stt,))
                vts(stt[:, 2, :], stt[:, 0, :], 1.0 / 64, None, ALU.mult, None, (b_stt,), (b_stt,))
                vtt(stt[:, 3, :], stt[:, 2, :], stt[:, 2, :], ALU.mult, (b_stt,), (b_stt,))
                vsts(stt[:, 4, :], stt[:, 1, :], 1.0 / 64, stt[:, 3, :], ALU.mult, ALU.subtract, (b_stt,), (b_stt,))
                rsqrt_act(stt[:, 4, :], stt[:, 4, :], (b_stt,), (b_stt,), 1.0, 64e-5)
                vtt(yt[:, :, :], yt[:, :, :], stt[:, 2, :].unsqueeze(2).broadcast_to([64, 16, 64]), ALU.subtract, (b_yt, b_stt), (b_yt,))
                vtt(yt[:, :, :], yt[:, :, :], stt[:, 4, :].unsqueeze(2).broadcast_to([64, 16, 64]), ALU.mult, (b_yt, b_stt), (b_yt,))
                for c in range(16):
                    bk = 7 if c < 8 else 0
                    tr(ps[bk][0:64, (c % 8) * 64:(c % 8) * 64 + 64], yt[:, c, :], id64, (b_yt, b_cst), (b_ps[bk],))
                vts(yf[:, 0:512], ps[7][0:64, 0:512], hp0[:, H, 0:1], hp0[:, H, 1:2], ALU.mult, ALU.add, (b_ps[7], b_hp0), (b_yf,))
                vts(yf[:, 512:1024], ps[0][0:64, 0:512], hp0[:, H, 0:1], hp0[:, H, 1:2], ALU.mult, ALU.add, (b_ps[0], b_hp0), (b_yf,))
                vtt(yf[:, :], yf[:, :], bnt[:, :], ALU.add, (b_yf, b_bnt), (b_yf,))
                pb_ = (H % 2) * 64
                vtt(mixT[pb_:pb_ + 64, H // 2, :], yf[:, :], gtt[:, :], ALU.mult, (b_yf, b_gtt), (b_mix[H // 2],))

    for l in range(depth):
        pc = l * NCL
        c_ln1, c_ln2, c_mu, c_w0, c_a0, c_kk, c_ka, c_rk, c_lxg, c_lxb, c_ag, c_gg, c_sk = (
            pc, pc + 16, pc + 32, pc + 52, pc + 58, pc + 64, pc + 70, pc + 76, pc + 82, pc + 88, pc + 94, pc + 100, pc + 104)
        S.dma("sp", lora[:], lora_d.ap()[l], (), (b_lora,))
        S.dma("sp", gup[:], gup_d.ap()[l], (), (b_gup,))
        skip_mix = dbg == "nomix"
        if dbg and "dump" in dbg:
            mixers(l, (c_ln1, c_ln2, c_mu, c_w0, c_a0, c_kk, c_ka, c_rk, c_lxg, c_lxb, c_ag, c_gg, c_sk))
            break
        if not skip_mix:
            mixers(l, (c_ln1, c_ln2, c_mu, c_w0, c_a0, c_kk, c_ka, c_rk, c_lxg, c_lxb, c_ag, c_gg, c_sk))
        if not skip_mix:
            for dc in range(16):
                w, bw = load_w(wtile("out", l, dc, 16), 128, 16, b_wga["out"][l])
                for tt in range(2):
                    pb = tt
                    for mc in range(16):
                        P(lambda e, mc=mc, tt=tt, w=w, pb=pb: e.matmul(ps[pb][:, :], lhsT=w[:, mc, 0:128], rhs=mixT[:, mc, tt * 512:(tt + 1) * 512],
                                                                    start=(mc == 0), stop=(mc == 15)), (bw, b_mix[mc]), (b_ps[pb],))
                    V(lambda e, dc=dc, tt=tt, pb=pb: e.tensor_tensor(out=xT[:, dc, tt * 512:(tt + 1) * 512], in0=xT[:, dc, tt * 512:(tt + 1) * 512],
                                                                    in1=ps[pb][:, :], op=ALU.add), (b_ps[pb], b_x[dc]), (b_x[dc],))
        S.barrier()
        with contextlib.ExitStack() as stk:
            h2 = sb("h2T", [128, 16, TT], BF16, stk); b_h2 = [Buf() for _ in range(16)]
            hact = sb("hact", [128, 8, TT], BF16, stk); b_act = [Buf() for _ in range(8)]
            tm = norm_tmps(stk, "n2")
            rl = [sb("rl%d" % i, [128, 512], F32, stk) for i in range(2)]; b_rl = [Buf(), Buf()]
            for tt in range(2):
                rmsnorm(tm, lambda kc, tt=tt: xT[:, kc, tt * 512:(tt + 1) * 512], lambda kc: b_x[kc], 512, c_ln2,
                        lambda kc, tt=tt: h2[:, kc, tt * 512:(tt + 1) * 512], lambda kc: b_h2[kc])
            cnt = 0
            for qd in range(8):
                for hc in range(8):
                    w, bw = load_w(wtile("up", l, qd * 8 + hc, 16), 128, 16, b_wga["up"][l])
                    for tt in range(2):
                        pb = cnt % 2; cnt += 1
                        for kc in range(16):
                            P(lambda e, kc=kc, tt=tt, w=w, pb=pb: e.matmul(ps[pb][:, :], lhsT=w[:, kc, 0:128], rhs=h2[:, kc, tt * 512:(tt + 1) * 512],
                                                                        start=(kc == 0), stop=(kc == 15)), (bw, b_h2[kc]), (b_ps[pb],))
                        A(lambda e, pb=pb: e.activation(out=rl[pb][:, :], in_=ps[pb][:, :], func=AF.Relu), (b_ps[pb],), (b_rl[pb],))
                        V(lambda e, pb=pb, hc=hc, tt=tt: e.tensor_tensor(out=hact[:, hc, tt * 512:(tt + 1) * 512], in0=rl[pb][:, :], in1=rl[pb][:, :], op=ALU.mult),
                          (b_rl[pb],), (b_act[hc],))
                for dc in range(16):
                    w, bw = load_w(wtile("dn", l, qd * 16 + dc, 8), 128, 8, b_wga["dn"][l])
                    for tt in range(2):
                        pb = cnt % 2; cnt += 1
                        for hc in range(8):
                            P(lambda e, hc=hc, tt=tt, w=w, pb=pb: e.matmul(ps[pb][:, :], lhsT=w[:, hc, 0:128], rhs=hact[:, hc, tt * 512:(tt + 1) * 512],
                                                                        start=(hc == 0), stop=(hc == 7)), (bw, b_act[hc]), (b_ps[pb],))
                        V(lambda e, dc=dc, tt=tt, pb=pb: e.tensor_tensor(out=xT[:, dc, tt * 512:(tt + 1) * 512], in0=xT[:, dc, tt * 512:(tt + 1) * 512],
                                                                        in1=ps[pb][:, :], op=ALU.add), (b_ps[pb], b_x[dc]), (b_x[dc],))
            S.barrier()
    with contextlib.ExitStack() as stk:
        ot = sb("ot", [128, 16, 512], F32, stk); b_ot = [Buf() for _ in range(16)]
        tm = norm_tmps(stk, "nf")
        ov = out_d.ap().rearrange("(k p) t -> p k t", p=128)
        c_lnf = L * NCL
        for tt in range(2):
            if dbg and "dump" in dbg:
                for kc in range(16):
                    V(lambda e, kc=kc, tt=tt: e.tensor_copy(out=ot[:, kc, :], in_=mixT[:, kc, tt * 512:(tt + 1) * 512]), (b_mix[kc],), (b_ot[kc],))
            else:
                rmsnorm(tm, lambda kc, tt=tt: xT[:, kc, tt * 512:(tt + 1) * 512], lambda kc: b_x[kc], 512, c_lnf,
                        lambda kc: ot[:, kc, :], lambda kc: b_ot[kc])
            for kc in range(16):
                S.dma("sp", ov[:, kc, tt * 512:(tt + 1) * 512], ot[:, kc, :], (b_ot[kc],), ())
        S.barrier()
    S.emit()
    es.close()
    return nc


def _prep(inputs, depth=L):
    f = lambda a: np.ascontiguousarray(np.asarray(a, dtype=np.float32))
    w_in = f(inputs["w_in"])
    win_r = np.empty((L, D * 4864), np.float32)
    for l in range(L):
        for g, cols in enumerate(GROUPS):
            blk = w_in[l][:, cols].reshape(16, 128, len(cols)).transpose(1, 0, 2)
            win_r[l, GOFF[g]:GOFF[g + 1]] = blk.reshape(-1)
    def tile_cols(w):
        Lx, K, N = w.shape
        return np.ascontiguousarray(w.reshape(Lx, K // 128, 128, N // 128, 128).transpose(0, 3, 2, 1, 4)).reshape(Lx, N // 128, 128, (K // 128) * 128)
    wout_r = tile_cols(f(inputs["w_out"]))
    wup_r = tile_cols(f(inputs["w_ffn_up"]))
    wd = f(inputs["w_ffn_down"]).reshape(L, 8, 1024, D)
    wdn_r = np.stack([tile_cols(wd[:, q]) for q in range(8)], axis=1).reshape(L, 128, 128, 1024)
    pcol = np.zeros((128, L * NCL + 16), np.float32)
    col = lambda v: f(v).reshape(-1, 128).T
    for l in range(L):
        pc = l * NCL
        pcol[:, pc:pc + 16] = col(inputs["ln1_g"][l])
        pcol[:, pc + 16:pc + 32] = col(inputs["ln2_g"][l])
        mu = f(inputs["rwkv_mu"][l])
        for i in range(6):
            for j in range(3):
                pcol[:, pc + 32 + 3 * i + j] = mu[768 * j + 128 * i:768 * j + 128 * i + 128]
        pcol[:, pc + 32 + 18] = mu[2304:2432]
        pcol[:, pc + 32 + 19] = mu[2432:2560]
        for k, name in enumerate(["rwkv_w0", "rwkv_a0", "rwkv_k_k", "rwkv_k_a", "rwkv_r_k", "rwkv_lnx_g", "rwkv_lnx_b", "attn_norm_g"]):
            pcol[:, pc + 52 + 6 * k:pc + 58 + 6 * k] = col(np.asarray(inputs[name][l]).reshape(-1))
        pcol[:, pc + 100:pc + 104] = col(inputs["gm_norm_g"][l])
        pcol[:, pc + 104:pc + 116] = f(inputs["attn_sinks"][l])[None, :]
    pcol[:, L * NCL:] = col(inputs["lnf_g"])
    pbc = np.zeros((L, 128, 1536), np.float32)
    pbc[:, :, 0:512] = f(inputs["gm_ln_g"])[:, None, :]
    pbc[:, :, 512:1024] = f(inputs["gm_ln_b"])[:, None, :]
    pbc[:, :, 1024:1536] = f(inputs["gm_bs"]).reshape(L, 1, 512)
    lora = np.concatenate([f(inputs["rwkv_decay_up"]), f(inputs["rwkv_a_up"])], axis=1)
    gup = f(inputs["rwkv_g_up"])
    gmw = np.ascontiguousarray(f(inputs["gm_ws"]).transpose(0, 3, 1, 2)).reshape(L, 128, 512)
    cst = np.zeros((128, 1088), np.float32)
    cst[:, 0:128] = np.eye(128)
    cst[0:64, 128:192] = 1; cst[64:128, 192:256] = 1
    s = np.arange(128)[:, None] % 64; t = np.arange(64)[None, :]
    cst[:, 256:320] = s < t; cst[:, 320:384] = s <= t
    cst[0:64, 384:448] = np.arange(64)[None, :] < np.arange(64)[:, None]
    k_ = np.arange(128)[:, None]; q_ = np.arange(128)[None, :]
    cst[:, 448:576] = k_ <= q_
    cst[:, 576:704] = k_ > q_
    if os.environ.get('K_ZMASK'):
        cst[:, 256:448] = 0
    cst[:, 704:960] = (np.arange(256) % 64 != 0)[None, :]
    cst[:, 960:1088] = 1
    xs = f(inputs["x"])[0]
    maps = []
    for c in range(NCORES):
        sel = np.zeros((128, 17), np.float32)
        if c > 0:
            sel[:, c - 1] = 1; sel[:, 8] = 1
        sel[:, 9 + c] = 1
        sh = lambda w, cols: np.ascontiguousarray(w.reshape(L, NCORES, -1, cols)[:depth, c])
        maps.append({"xT": np.ascontiguousarray(xs[c * TT:(c + 1) * TT].T), "wsh_in": sh(win_r, 2048), "wsh_out": sh(wout_r, 2048),
                     "wsh_up": sh(wup_r, 2048), "wsh_dn": sh(wdn_r, 1024), "pcol": pcol, "pbc": pbc, "lora": lora, "gup": gup, "gmw": gmw, "consts": cst, "sel": sel})
    return maps


_NC = {}
def kernel(**inputs):
    depth = int(os.environ.get("K_DEPTH", L)); dbg = os.environ.get("K_DBG") or None
    key = (depth, dbg)
    if key not in _NC:
        _NC[key] = build(depth, dbg)
    maps = _prep(inputs, depth)
    res = run_bass_kernel_spmd(_NC[key], maps, core_ids=list(range(NCORES)))
    out = np.concatenate([np.asarray(r["outT"]).T for r in res.results], axis=0)
    return out[None].astype(np.float32)
```
